# Optimizing a Trainium2 kernel written in Bass

```python
import jax, jax.numpy as jnp
from jax import lax
import numpy as np

D_MODEL = 2048
BATCH = 2
SEQ = 4096
DEPTH = 1

CHUNK = 64
POOL_WINDOWS = (2, 4, 8, 16)
POOL_WIDTH = D_MODEL // 2
POOL_GROUP = POOL_WIDTH // len(POOL_WINDOWS)
RET_HEADS = 8
RET_QK_DIM = 128
RET_V_DIM = 256
RET_QK_WIDTH = RET_HEADS * RET_QK_DIM
RET_V_WIDTH = RET_HEADS * RET_V_DIM
D_FF = 5632
CONV_WIDTH = 3
ROPE_BASE = 10000.0
RMS_EPS = 1e-6
GN_EPS = 1e-5
IN_SPLITS = (POOL_WIDTH, RET_QK_WIDTH, RET_QK_WIDTH, RET_V_WIDTH, RET_V_WIDTH, D_MODEL, D_MODEL)
IN_WIDTH = POOL_WIDTH + 2 * RET_QK_WIDTH + 2 * RET_V_WIDTH + 2 * D_MODEL

kernel_name = "hybrid_pool_retention_convglu_block"


def rmsnorm(x, g):
    xf = x.astype(jnp.float32)
    y = xf * lax.rsqrt(jnp.mean(xf * xf, axis=-1, keepdims=True) + RMS_EPS)
    return (y * g.astype(jnp.float32)).astype(x.dtype)


def pool_branch(p, w_group, scale):
    B, T, _ = p.shape
    pf = p.astype(jnp.float32)
    csum = jnp.pad(jnp.cumsum(pf, axis=1), ((0, 0), (1, 0), (0, 0)))
    t = jnp.arange(T)
    means = []
    for gi, w in enumerate(POOL_WINDOWS):
        cg = csum[:, :, gi * POOL_GROUP:(gi + 1) * POOL_GROUP]
        upper = cg[:, 1:]
        lower = jnp.take(cg, jnp.maximum(t + 1 - w, 0), axis=1)
        cnt = jnp.minimum(t + 1, w).astype(jnp.float32)
        means.append((upper - lower) / cnt[None, :, None])
    pooled = jnp.concatenate(means, axis=-1) - pf
    pooled = pooled.reshape(B, T, len(POOL_WINDOWS), POOL_GROUP)
    mixed = jnp.einsum('btgc,gcd->btgd', pooled, w_group.astype(jnp.float32))
    mixed = mixed.reshape(B, T, POOL_WIDTH) * scale.astype(jnp.float32)
    return mixed.astype(p.dtype)


def rotary(x):
    T, d = x.shape[1], x.shape[-1]
    half = d // 2
    inv = ROPE_BASE ** (-jnp.arange(half, dtype=jnp.float32) / half)
    ang = jnp.arange(T, dtype=jnp.float32)[:, None] * inv[None, :]
    cos = jnp.cos(ang)[None, :, None, :]
    sin = jnp.sin(ang)[None, :, None, :]
    x1, x2 = x[..., :half], x[..., half:]
    return jnp.concatenate([x1 * cos - x2 * sin, x2 * cos + x1 * sin], axis=-1)


def retention(q, k, v):
    B, T, H, dk = q.shape
    dv = v.shape[-1]
    N = T // CHUNK
    log_g = jnp.log(1.0 - 2.0 ** (-5.0 - jnp.arange(H, dtype=jnp.float32)))
    j = jnp.arange(CHUNK, dtype=jnp.float32)
    intra_decay = jnp.exp(jnp.abs(j[:, None] - j[None, :])[None] * log_g[:, None, None])
    q_decay = jnp.exp((j + 1.0)[:, None] * log_g[None, :])
    k_decay = jnp.exp((CHUNK - 1.0 - j)[:, None] * log_g[None, :])
    chunk_decay = jnp.exp(CHUNK * log_g)

    qc = q.reshape(B, N, CHUNK, H, dk)
    kc = k.reshape(B, N, CHUNK, H, dk)
    vc = v.reshape(B, N, CHUNK, H, dv)

    scores = jnp.einsum('bnihd,bnjhd->bnhij', qc, kc) * intra_decay[None, None]
    intra = jnp.einsum('bnhij,bnjhe->bnihe', scores, vc)

    def step(state, inp):
        qn, kn, vn = inp
        cross = jnp.einsum('bihd,bhde->bihe', qn * q_decay[None, :, :, None], state)
        state = state * chunk_decay[None, :, None, None] + jnp.einsum(
            'bjhd,bjhe->bhde', kn * k_decay[None, :, :, None], vn)
        return state, cross

    s0 = jnp.zeros((B, H, dk, dv), jnp.float32)
    _, cross = lax.scan(step, s0, (qc.swapaxes(0, 1), kc.swapaxes(0, 1), vc.swapaxes(0, 1)))
    out = intra + cross.swapaxes(0, 1)
    return out.reshape(B, T, H, dv)


def retention_branch(q, k, v, g, gn_gain):
    B, T, _ = q.shape
    qf = rotary(q.astype(jnp.float32).reshape(B, T, RET_HEADS, RET_QK_DIM))
    kf = rotary(k.astype(jnp.float32).reshape(B, T, RET_HEADS, RET_QK_DIM)) * (RET_QK_DIM ** -0.5)
    vf = v.astype(jnp.float32).reshape(B, T, RET_HEADS, RET_V_DIM)
    o = retention(qf, kf, vf)
    mu = jnp.mean(o, axis=-1, keepdims=True)
    var = jnp.mean(jnp.square(o - mu), axis=-1, keepdims=True)
    o = (o - mu) * lax.rsqrt(var + GN_EPS)
    o = o.reshape(B, T, RET_V_WIDTH) * gn_gain.astype(jnp.float32)
    return (jax.nn.silu(g.astype(jnp.float32)) * o).astype(q.dtype)


def conv_glu(h, w_up, conv_w, conv_b, w_down):
    ug = h @ w_up
    gate, up = ug[..., :D_FF], ug[..., D_FF:]
    gate = lax.conv_general_dilated(
        gate, conv_w[:, None, :].astype(gate.dtype), window_strides=(1,),
        padding=((CONV_WIDTH - 1, 0),), dimension_numbers=('NWC', 'WIO', 'NWC'),
        feature_group_count=D_FF) + conv_b
    return (jax.nn.silu(gate) * up) @ w_down


def setup_inputs(seed: int = 0) -> dict:
    key = jax.random.key(seed)
    ks = jax.random.split(key, 16)
    f32 = jnp.float32

    def w(k, shape, fan_in):
        return jax.random.normal(k, shape, f32) * (fan_in ** -0.5)

    def gain(k, shape):
        return 1.0 + 0.02 * jax.random.normal(k, shape, f32)

    return {
        "x": jax.random.normal(ks[0], (BATCH, SEQ, D_MODEL), f32),
        "norm_mix": gain(ks[1], (DEPTH, D_MODEL)),
        "w_in": w(ks[2], (DEPTH, D_MODEL, IN_WIDTH), D_MODEL),
        "w_pool_group": w(ks[3], (DEPTH, len(POOL_WINDOWS), POOL_GROUP, POOL_GROUP), POOL_GROUP),
        "pool_scale": gain(ks[4], (DEPTH, POOL_WIDTH)),
        "w_pool_proj": w(ks[5], (DEPTH, POOL_WIDTH, D_MODEL), POOL_WIDTH),
        "ret_gn_gain": gain(ks[6], (DEPTH, RET_V_WIDTH)),
        "w_ret_proj": w(ks[7], (DEPTH, RET_V_WIDTH, D_MODEL), RET_V_WIDTH),
        "w_out": w(ks[8], (DEPTH, D_MODEL, D_MODEL), D_MODEL),
        "norm_ffn": gain(ks[9], (DEPTH, D_MODEL)),
        "w_up": w(ks[10], (DEPTH, D_MODEL, 2 * D_FF), D_MODEL),
        "conv_w": w(ks[11], (DEPTH, CONV_WIDTH, D_FF), CONV_WIDTH),
        "conv_b": 0.02 * jax.random.normal(ks[12], (DEPTH, D_FF), f32),
        "w_down": w(ks[13], (DEPTH, D_FF, D_MODEL), D_FF),
        "norm_final": gain(ks[14], (D_MODEL,)),
    }


def reference(x, norm_mix, w_in, w_pool_group, pool_scale, w_pool_proj, ret_gn_gain,
              w_ret_proj, w_out, norm_ffn, w_up, conv_w, conv_b, w_down, norm_final):
    split_idx = np.cumsum(IN_SPLITS)[:-1].tolist()
    h = x
    for l in range(DEPTH):
        u = rmsnorm(h, norm_mix[l])
        z = u @ w_in[l]
        p, q, k, v, g_ret, g_pool_br, g_ret_br = jnp.split(z, split_idx, axis=-1)
        y_pool = pool_branch(p, w_pool_group[l], pool_scale[l]) @ w_pool_proj[l]
        y_ret = retention_branch(q, k, v, g_ret, ret_gn_gain[l]) @ w_ret_proj[l]
        merged = jax.nn.sigmoid(g_pool_br) * y_pool + jax.nn.sigmoid(g_ret_br) * y_ret
        h = h + merged @ w_out[l]
        h = h + conv_glu(rmsnorm(h, norm_ffn[l]), w_up[l], conv_w[l], conv_b[l], w_down[l])
    return rmsnorm(h, norm_final)
```

```python
from contextlib import ExitStack

import numpy as np
import concourse.bass as bass
import concourse.mybir as mybir
from concourse.bass_utils import run_bass_kernel_spmd

F32 = mybir.dt.float32
BF16 = mybir.dt.bfloat16
ALU = mybir.AluOpType
AF = mybir.ActivationFunctionType

NCORES = 8
TOK = 1024
NT = 8
D = 2048
KT = 16
DFF = 5632
NJ = 44
H = 8
NSLOT = 4
SLOT = 4096
NPS = 5
DN_G = 8
P3ONLY = False
P3FLAGS = set()


def I(name, *args, **kwargs):
    return (name, args, kwargs)


class Buf:
    __slots__ = ("w", "r")

    def __init__(self):
        self.w = None
        self.r = []


class Prog:
    ENGS = ("pe", "act", "dve", "pool", "sp")

    def __init__(self, n_dma_sems=16):
        self.ops = {e: [] for e in self.ENGS}
        self.cnt = {e: 0 for e in self.ENGS}
        self.seen = {e: {} for e in self.ENGS}
        self.n_dma = n_dma_sems
        self.dma_cnt = [0] * n_dma_sems
        self.dma_next = 0
        self.cc_cnt = 0
        self.bufs = {}
        self.muted = False

    def buf(self, name):
        b = self.bufs.get(name)
        if b is None:
            b = self.bufs[name] = Buf()
        return b

    def _need(self, eng, ev, waits):
        if ev is None:
            return
        k, v = ev
        if self.seen[eng].get(k, 0) >= v:
            return
        self.seen[eng][k] = v
        waits[k] = max(waits.get(k, 0), v)

    def _deps(self, eng, reads, writes, own_key):
        waits = {}
        for n in reads:
            b = self.buf(n)
            if b.w is not None:
                if b.w[0] == own_key and eng == "pe":
                    continue
                self._need(eng, b.w, waits)
        for n in writes:
            b = self.buf(n)
            if b.w is not None and b.w[0] != own_key:
                self._need(eng, b.w, waits)
            for ev in b.r:
                if ev[0] != own_key:
                    self._need(eng, ev, waits)
        return waits

    def _commit(self, ev, reads, writes):
        for n in reads:
            self.buf(n).r.append(ev)
        for n in writes:
            b = self.buf(n)
            b.w = ev
            b.r = []

    def op(self, eng, fns, reads=(), writes=()):
        if self.muted:
            return None
        if isinstance(fns, tuple):
            fns = [fns]
        waits = self._deps(eng, reads, writes, eng)
        self.cnt[eng] += 1
        ev = (eng, self.cnt[eng])
        self._commit(ev, reads, writes)
        self.ops[eng].append((waits, fns, eng, 1))
        return ev

    def dma(self, eng, fn, reads=(), writes=()):
        if self.muted:
            return None
        k = self.dma_next
        self.dma_next = (k + 1) % self.n_dma
        key = "d%d" % k
        waits = self._deps(eng, reads, writes, None)
        if self.dma_cnt[k] > 0:
            self._need(eng, (key, self.dma_cnt[k]), waits)
        self.dma_cnt[k] += 16
        ev = (key, self.dma_cnt[k])
        self._commit(ev, reads, writes)
        self.ops[eng].append((waits, [fn], key, 16))
        return ev

    def cc(self, eng, fn, reads=(), writes=()):
        if self.muted:
            return None
        waits = self._deps(eng, reads, writes, None)
        if self.cc_cnt > 0:
            self._need(eng, ("cc", self.cc_cnt), waits)
        self.cc_cnt += 1
        ev = ("cc", self.cc_cnt)
        self._commit(ev, reads, writes)
        self.ops[eng].append((waits, [fn], "cc", 1))
        return ev

    def _all_events(self):
        evs = [(e, self.cnt[e]) for e in self.ENGS if self.cnt[e] > 0]
        evs += [("d%d" % k, c) for k, c in enumerate(self.dma_cnt) if c > 0]
        if self.cc_cnt:
            evs.append(("cc", self.cc_cnt))
        return evs

    def barrier(self):
        if self.muted:
            return
        evs = self._all_events()
        for e in self.ENGS:
            waits = {}
            for ev in evs:
                if ev[0] != e:
                    self._need(e, ev, waits)
            if waits:
                self.ops[e].append((waits, [], None, 0))

    def final_wait(self, eng="sp"):
        waits = {}
        for ev in self._all_events():
            if ev[0] != eng:
                self._need(eng, ev, waits)
        self.ops[eng].append((waits, [], None, 0))

    def replay(self, block, sems):
        def run(eng_name):
            def body(e):
                for (waits, fns, inc_key, inc_amt) in self.ops[eng_name]:
                    for k, v in waits.items():
                        e.wait_ge(sems[k], v)
                    last = None
                    for (nm, a, kw) in fns:
                        last = getattr(e, nm)(*a, **kw)
                    if last is not None and inc_key is not None:
                        last.then_inc(sems[inc_key], inc_amt)
            return body
        block.tensor(run("pe"))
        block.scalar(run("act"))
        block.vector(run("dve"))
        block.gpsimd(run("pool"))
        block.sync(run("sp"))


CST_FIELDS = [
    ("g1", 16), ("g2", 16), ("pscale", 8), ("gng", 16), ("cw", NJ * 3), ("cb", NJ),
    ("cos", NT * 128), ("sin", NT * 128), ("DT", H * 128), ("QD", H * 128),
    ("kdec", H), ("kdecL", NT * H), ("cd2", H), ("coef", NCORES * H), ("invc", 64), ("sel", NCORES),
]
CST_OFF = {}
_o = 0
for _n, _w in CST_FIELDS:
    CST_OFF[_n] = (_o, _w)
    _o += _w
CST_W = _o


def dn_groups():
    gs = []
    j = 0
    while j < NJ:
        g = min(DN_G, NJ - j)
        gs.append((j, g))
        j += g
    return gs


def build_nc(dbg=None):
    nc = bass.Bass("TRN2", target_bir_lowering=False)
    P = Prog()

    def din(name, shape):
        return nc.dram_tensor(name, shape, F32, kind="ExternalInput").ap()

    x_d = din("x", [TOK, D])
    xh_d = din("xh", [16, D])
    cst_d = din("cst", [128, CST_W])
    gfin_d = din("gfin", [128, D])
    HD = 2 if P3ONLY else H
    uT_in = din("uT_in", [128, KT * TOK]) if P3ONLY else None
    w_pin = din("w_pin", [8, 128, KT * 128]) if not P3ONLY else None
    w_gp = din("w_gp", [16, 128, KT * 128]) if not P3ONLY else None
    w_gr = din("w_gr", [16, 128, KT * 128]) if not (dbg is not None and dbg[0] <= 3) else None
    w_qk = din("w_qk", [HD, 128, KT * 256])
    w_v = din("w_v", [HD, 128, KT * 256])
    w_g = din("w_g", [HD, 128, KT * 256])
    w_pg = din("w_pg", [4, 128, 2 * 256]) if not P3ONLY else None
    w_pp = din("w_pp", [16, 128, 8 * 128]) if not P3ONLY else None
    lite = dbg is not None and dbg[0] <= 3
    if lite:
        w_rp = w_o = w_up = w_dn = None
    else:
        w_rp = din("w_rp", [16, 128, KT * 128])
        w_o = din("w_o", [8, 128, KT * 256])
        w_up = din("w_up", [NJ, 128, KT * 256])
        w_dn = din("w_dn", [4, 128, NJ * 512])
    out_d = nc.dram_tensor("out", [TOK, D], F32, kind="ExternalOutput").ap()
    dbg_d = None
    if dbg is not None:
        dbg_d = nc.dram_tensor("dbg", [4 * TOK, D], F32, kind="ExternalOutput").ap()
    st_b = [nc.dram_tensor("st_b%d" % h, [128, 256], F32).ap() for h in range(H)]
    st_g = [nc.dram_tensor("st_g%d" % h, [NCORES * 128, 256], F32).ap() for h in range(H)]
    hl_b = nc.dram_tensor("hl_b", [128, 256], F32).ap()
    hl_g = nc.dram_tensor("hl_g", [NCORES * 128, 256], F32).ap()

    with ExitStack() as es:
        def sb(name, shape, dt, stack=es):
            return stack.enter_context(nc.sbuf_tensor("sb_" + name, shape, dt))

        sems = {}
        for k in list(Prog.ENGS) + ["d%d" % i for i in range(P.n_dma)] + ["cc"]:
            sems[k] = es.enter_context(nc.semaphore("s_" + k))

        cst = sb("cst", [128, CST_W], F32)
        ident = sb("ident", [128, 128], BF16)
        identf = sb("identf", [128, 128], F32)
        slots = sb("slots", [128, NSLOT, SLOT], BF16)
        R64 = sb("R64", [128, NT * D], F32)
        stat = sb("stat", [128, 64], F32)
        dtmp = sb("dtmp", [128, 2048], F32) if dbg is not None else None
        psb = [es.enter_context(nc.psum_tensor("psb%d" % i, [128, 512], F32)) for i in range(NPS)]
        plb = es.enter_context(nc.psum_tensor("plb", [128, 512], F32))
        tpb = [es.enter_context(nc.psum_tensor("tpb%d" % i, [128, 8, 128], BF16)) for i in range(2)]
        block = es.enter_context(nc.Block())

        def C(name, lo=0, hi=None):
            o, w = CST_OFF[name]
            hi = w if hi is None else hi
            return cst[:, o + lo:o + hi]

        R64b = R64[:, 0:NT * D].bitcast(BF16)
        uT = R64b[:, 0:KT * TOK].rearrange("p (k t) -> p k t", t=TOK)
        rT = R64b[:, KT * TOK:2 * KT * TOK].rearrange("p (k t) -> p k t", t=TOK)
        hres = R64[:, :].rearrange("p (i d) -> p i d", d=D)

        st = {"ps": 0, "tp": 0, "w": 0}

        def ps_next():
            k = st["ps"] % NPS
            st["ps"] += 1
            return psb[k], "ps%d" % k

        def tp_next():
            k = st["tp"] % 2
            st["tp"] += 1
            return tpb[k], "tp%d" % k

        wplan = []

        def plan():
            for b in range(8):
                if P3ONLY:
                    break
                wplan.append((w_pin[b], KT, 128))
                if b % 2 == 1:
                    wplan.append((w_pg[b // 2], 2, 256))
            for f in range(16):
                if P3ONLY:
                    break
                wplan.append((w_pp[f], 8, 128))
                wplan.append((w_gp[f], KT, 128))
            for h in range(HD):
                wplan.append((w_qk[h], KT, 256))
                wplan.append((w_v[h], KT, 256))
                wplan.append((w_g[h], KT, 256))
            if lite:
                return
            for f in range(16):
                wplan.append((w_rp[f], KT, 128))
                wplan.append((w_gr[f], KT, 128))
            for n in range(8):
                wplan.append((w_o[n], KT, 256))
            for (j0, g) in dn_groups():
                for j in range(j0, j0 + g):
                    wplan.append((w_up[j], KT, 256))
                for n in range(4):
                    wplan.append((w_dn[n, :, j0 * 512:(j0 + g) * 512], g, 512))
        plan()
        wstate = {"issued": 0, "used": 0}

        def w_issue():
            n = wstate["issued"]
            src, kk, wd = wplan[n]
            k = n % NSLOT
            view = slots[:, k, 0:kk * wd]
            P.dma("pool", I("dma_start", out=view, in_=src), writes=["ws%d" % k])
            wstate["issued"] += 1

        def w_get(expect_w):
            if P.muted:
                return slots[:, 0, 0:SLOT].rearrange("p (k w) -> p k w", w=expect_w), "ws0"
            n = wstate["used"]
            wstate["used"] += 1
            while wstate["issued"] < min(len(wplan), n + 2):
                w_issue()
            src, kk, wd = wplan[n]
            assert wd == expect_w, (n, wd, expect_w)
            k = n % NSLOT
            return slots[:, k, 0:kk * wd].rearrange("p (k w) -> p k w", w=wd), "ws%d" % k

        P.dma("sp", I("dma_start", out=cst[:], in_=cst_d[:, :]), writes=["cst"])
        P.op("pool", I("memset", identf[:], 0.0), writes=["identf"])
        P.op("pool", I("affine_select", out=identf[:], in_=identf[:], compare_op=ALU.not_equal, fill=1.0,
                                               base=0, pattern=[[-1, 128]], channel_multiplier=1),
             reads=["identf"], writes=["identf"])
        P.op("dve", I("tensor_copy", out=ident[:], in_=identf[:]), reads=["identf"], writes=["ident"])
        P.op("dve", I("memset", stat[:], 0.0), writes=["stat"])

        if P3ONLY:
            P.muted = True

        def uT_b(half):
            return ["uT_%d" % i for i in range(half * 4, half * 4 + 4)]

        def rT_b(half):
            return ["rT_%d" % i for i in range(half * 4, half * 4 + 4)]

        def norm_transpose(src_ap, rows, src_names, xn_t, xn_name, junk_t, statcol, gname, dst_fn, dst_name):
            sc = stat[0:rows, statcol:statcol + 1]
            sn = "st%d" % statcol
            P.op("act", I("activation", out=junk_t[0:rows, :], in_=src_ap, func=AF.Square, accum_out=sc),
                 reads=src_names + ["stat"], writes=["junk", sn])
            P.op("dve", I("tensor_scalar", out=sc, in0=sc, scalar1=1.0 / D, scalar2=1e-6, op0=ALU.mult, op1=ALU.add),
                 reads=[sn], writes=[sn])
            P.op("act", I("sqrt", out=sc, in_=sc), reads=[sn], writes=[sn])
            P.op("dve", I("reciprocal", out=sc, in_=sc), reads=[sn], writes=[sn])
            P.op("dve", I("tensor_scalar_mul", out=xn_t[0:rows, :], in0=src_ap, scalar1=sc),
                 reads=src_names + [sn], writes=[xn_name])
            for half in range(2):
                tpk, tpn = tp_next()
                P.op("pe", [(I("transpose", out=tpk[:, kk, 0:rows],
                                                                    in_=xn_t[0:rows, (half * 8 + kk) * 128:(half * 8 + kk + 1) * 128],
                                                                    identity=ident[0:rows, 0:rows])) for kk in range(8)],
                     reads=[xn_name, "ident"], writes=[tpn])
                gb = C(gname, half * 8, half * 8 + 8).unsqueeze(2).to_broadcast([128, 8, rows])
                P.op("dve", I("tensor_tensor", out=dst_fn(half), in0=tpk[:, :, 0:rows], in1=gb, op=ALU.mult),
                     reads=[tpn, "cst"], writes=[dst_name])

        class _Stop(Exception):
            pass

        dumps = {}

        def stop_here(k):
            if dbg is None or P.muted:
                return
            if k in dbg[1]:
                nm, off = dbg[1][k]
                dumps[nm](off)
            if dbg[0] == k:
                P.barrier()
                P.muted = True

        dbg2 = dbg_d.rearrange("(p a) d -> p (a d)", p=128) if dbg is not None else None

        def dump(src, ncols, off, names):
            c0 = 0
            while c0 < ncols:
                n = min(2048, ncols - c0)
                P.op("dve", I("tensor_copy", out=dtmp[:, 0:n], in_=src[:, c0:c0 + n]), reads=names, writes=["dtmp"])
                P.dma("sp", I("dma_start", out=dbg2[:, off + c0:off + c0 + n], in_=dtmp[:, 0:n]), reads=["dtmp"])
                c0 += n

        if True:
          if True:
            with ExitStack() as mx:
                mT = sb("mT", [128, KT, TOK], BF16, mx)
                dumps["uT"] = lambda off: dump(R64b[:, 0:KT * TOK], KT * TOK, off, ["uT_%d" % i for i in range(NT)])
                dumps["rT"] = lambda off: dump(R64b[:, KT * TOK:2 * KT * TOK], KT * TOK, off, ["rT_%d" % i for i in range(NT)])
                dumps["mT"] = lambda off: dump(mT[:, :, :].rearrange("p k t -> p (k t)"), KT * TOK, off, ["mT_%d_%d" % (f, hh) for f in range(16) for hh in range(2)])
                dumps["h"] = lambda off: dump(R64[:, :], NT * D, off, ["h_%d" % i for i in range(NT)])
                uhT = sb("uhT", [128, KT, 16], BF16, mx)

                with ExitStack() as p1:
                    xt = [sb("xt%d" % i, [128, D], F32, p1) for i in range(2)]
                    xn = [sb("xn%d" % i, [128, D], BF16, p1) for i in range(2)]
                    junk = sb("junk", [128, D], BF16, p1)
                    for i in range(NT + 1):
                        rows = 128 if i < NT else 16
                        src = x_d[i * 128:(i + 1) * 128, :] if i < NT else xh_d[:, :]
                        xtn = "xt%d" % (i % 2)
                        P.dma("sp", I("dma_start", out=xt[i % 2][0:rows, :], in_=src), writes=[xtn])
                        if i < NT:
                            dst_fn = (lambda half, i=i: uT[:, half * 8:(half + 1) * 8, i * 128:(i + 1) * 128])
                            dname = "uT_%d" % i
                        else:
                            dst_fn = (lambda half: uhT[:, half * 8:(half + 1) * 8, :])
                            dname = "uhT"
                        norm_transpose(xt[i % 2][0:rows, :], rows, [xtn], xn[i % 2], "xn%d" % (i % 2), junk, i, "g1", dst_fn, dname)
                    P.barrier()
                    stop_here(1)

                with ExitStack() as p2:
                    pf = sb("pf", [128, 16 + TOK], F32, p2)
                    sa = sb("sa", [128, 16 + TOK], F32, p2)
                    sbb = sb("sbb", [128, 16 + TOK], F32, p2)
                    t16 = sb("t16", [128, 16], F32, p2)
                    pooledT = sb("pooledT", [128, 2, TOK], BF16, p2)
                    mixedT = sb("mixedT", [128, 8, TOK], BF16, p2)
                    sig = [sb("sig%d" % i, [128, 512], F32, p2) for i in range(2)]
                    for b in range(8):
                        g = b // 2
                        bb = b % 2
                        wv, wn = w_get(128)
                        pss = []
                        for half in range(2):
                            pt, pn = ps_next()
                            P.op("pe", [(I("matmul", pt[:, :], lhsT=wv[:, kt, :], rhs=uT[:, kt, half * 512:(half + 1) * 512],
                                                                                  start=(kt == 0), stop=(kt == KT - 1))) for kt in range(KT)],
                                 reads=[wn] + uT_b(half), writes=[pn])
                            pss.append((pt, pn))
                        pt, pn = ps_next()
                        P.op("pe", [(I("matmul", pt[:, 0:16], lhsT=wv[:, kt, :], rhs=uhT[:, kt, :],
                                                                     start=(kt == 0), stop=(kt == KT - 1))) for kt in range(KT)],
                             reads=[wn, "uhT"], writes=[pn])
                        P.op("act", I("copy", out=pf[:, 0:16], in_=pt[:, 0:16]), reads=[pn], writes=["pf"])
                        P.op("act", I("copy", out=pf[:, 16:528], in_=pss[0][0][:, :]), reads=[pss[0][1]], writes=["pf"])
                        P.op("dve", I("tensor_copy", out=pf[:, 528:1040], in_=pss[1][0][:, :]), reads=[pss[1][1]], writes=["pf"])
                        cur, curn = pf, "pf"
                        tot = 0
                        tmps = [(sa, "sa"), (sbb, "sbb")]
                        for si, step in enumerate([1, 2, 4, 8][:g + 1]):
                            tot += step
                            nxt, nxtn = tmps[si % 2]
                            P.op("dve", I("tensor_tensor",
                                out=nxt[:, tot:1040], in0=cur[:, tot:1040], in1=cur[:, tot - step:1040 - step], op=ALU.add),
                                reads=[curn], writes=[nxtn])
                            cur, curn = nxt, nxtn
                        wdw = float(2 ** (g + 1))
                        P.op("dve", I("scalar_tensor_tensor",
                            out=pooledT[:, bb, :], in0=cur[:, 16:1040], scalar=1.0 / wdw, in1=pf[:, 16:1040], op0=ALU.mult, op1=ALU.subtract),
                            reads=[curn, "pf"], writes=["pooledT%d" % bb])
                        ic = C("invc", g * 16, g * 16 + 16)
                        P.op("dve", I("tensor_tensor", out=t16[:], in0=cur[:, 16:32], in1=ic, op=ALU.mult),
                             reads=[curn, "cst"], writes=["t16"])
                        P.op("dve", I("tensor_tensor", out=pooledT[:, bb, 0:16], in0=t16[:], in1=pf[:, 16:32], op=ALU.subtract),
                             reads=["t16", "pf"], writes=["pooledT%d" % bb])
                        if bb == 1:
                            wgv, wgn = w_get(256)
                            for m in range(2):
                                for half in range(2):
                                    pt, pn = ps_next()
                                    P.op("pe", [(I("matmul",
                                        pt[:, :], lhsT=wgv[:, kt, m * 128:(m + 1) * 128], rhs=pooledT[:, kt, half * 512:(half + 1) * 512],
                                        start=(kt == 0), stop=(kt == 1))) for kt in range(2)],
                                        reads=[wgn, "pooledT0", "pooledT1"], writes=[pn])
                                    col = g * 2 + m
                                    P.op("act", I("activation",
                                        out=mixedT[:, col, half * 512:(half + 1) * 512], in_=pt[:, :], func=AF.Copy, scale=C("pscale", col, col + 1)),
                                        reads=[pn, "cst"], writes=["mixedT"])
                    for f in range(16):
                        wpv, wpn = w_get(128)
                        wgv, wgn = w_get(128)
                        for half in range(2):
                            pa, pan = ps_next()
                            P.op("pe", [(I("matmul", pa[:, :], lhsT=wpv[:, kt, :], rhs=mixedT[:, kt, half * 512:(half + 1) * 512],
                                                                                  start=(kt == 0), stop=(kt == 7))) for kt in range(8)],
                                 reads=[wpn, "mixedT"], writes=[pan])
                            pb, pbn = ps_next()
                            P.op("pe", [(I("matmul", pb[:, :], lhsT=wgv[:, kt, :], rhs=uT[:, kt, half * 512:(half + 1) * 512],
                                                                                  start=(kt == 0), stop=(kt == KT - 1))) for kt in range(KT)],
                                 reads=[wgn] + uT_b(half), writes=[pbn])
                            sg_, sgn = sig[half], "sig%d" % half
                            P.op("act", I("activation", out=sg_[:], in_=pb[:, :], func=AF.Sigmoid), reads=[pbn], writes=[sgn])
                            P.op("dve", I("tensor_tensor",
                                out=mT[:, f, half * 512:(half + 1) * 512], in0=pa[:, :], in1=sg_[:], op=ALU.mult),
                                reads=[pan, sgn], writes=["mT_%d_%d" % (f, half)])
                    P.barrier()
                    stop_here(2)

                with ExitStack() as p3:
                    hb = []
                    for i in range(2):
                        hb.append(dict(
                            qT=sb("qT%d" % i, [128, TOK], BF16, p3), qdT=sb("qdT%d" % i, [128, TOK], BF16, p3),
                            kT=sb("kT%d" % i, [128, TOK], BF16, p3), kd=sb("kd%d" % i, [128, NT, 128], BF16, p3),
                            v=sb("v%d" % i, [128, NT, 256], BF16, p3), sg=sb("sg%d" % i, [128, NT, 256], BF16, p3)))
                    tA = [sb("tA%d" % i, [128, 256], F32, p3) for i in range(2)]
                    tB = [sb("tB%d" % i, [128, 256], F32, p3) for i in range(2)]
                    rq = [sb("rq%d" % i, [128, 256], BF16, p3) for i in range(2)]
                    kdl = [sb("kdl%d" % i, [128, 128], BF16, p3) for i in range(2)]
                    sTs = [sb("sTs%d" % i, [128, 128], BF16, p3) for i in range(2)]
                    Sf = sb("Sf", [128, 256], F32, p3)
                    Sb = [sb("Sb%d" % i, [128, 256], BF16, p3) for i in range(2)]
                    Lsb = sb("Lsb", [128, 256], F32, p3)
                    Gs = sb("Gs", [128, 4, 256], F32, p3)
                    on = [sb("on%d" % i, [128, 256], F32, p3) for i in range(2)]
                    rtm = [sb("rtm%d" % i, [128, 256], BF16, p3) for i in range(2)]
                    gjunk = sb("gjunk", [128, 256], F32, p3)
                    gst = sb("gst", [128, 2 * NT * 4], F32, p3)
                    cnt3 = {"rot": 0, "gn": 0}

                    def inproj(h):
                        B_ = hb[h % 2]
                        bn = "h%d_" % (h % 2)
                        wq, wqn = w_get(256)
                        wvv, wvn = w_get(256)
                        wgg, wgn = w_get(256)
                        pl, pln = plb, "plb"
                        for i in range(NT):
                            tok = slice(i * 128, (i + 1) * 128)
                            pq, pqn = ps_next()
                            P.op("pe", [(I("matmul", pq[:, 0:256], lhsT=uT[:, kt, tok], rhs=wq[:, kt, :],
                                                                                  start=(kt == 0), stop=(kt == KT - 1))) for kt in range(KT)],
                                 reads=[wqn, "uT_%d" % i], writes=[pqn])
                            pv, pvn = ps_next()
                            P.op("pe", [(I("matmul", pv[:, 0:256], lhsT=uT[:, kt, tok], rhs=wvv[:, kt, :],
                                                                                  start=(kt == 0), stop=(kt == KT - 1))) for kt in range(KT)]
                                 + [(I("matmul", pv[:, 256:512], lhsT=uT[:, kt, tok], rhs=wgg[:, kt, :],
                                                                               start=(kt == 0), stop=(kt == KT - 1))) for kt in range(KT)],
                                 reads=[wvn, wgn, "uT_%d" % i], writes=[pvn])
                            r = cnt3["rot"] % 2
                            cnt3["rot"] += 1
                            ps4 = pq[:, 0:256].rearrange("p (a b d) -> p a b d", a=2, b=2)
                            cosb = C("cos", i * 128, (i + 1) * 128).rearrange("p (a d) -> p a d", a=2).unsqueeze(2).to_broadcast([128, 2, 2, 64])
                            sinb = C("sin", i * 128, (i + 1) * 128).rearrange("p (a d) -> p a d", a=2).unsqueeze(2).to_broadcast([128, 2, 2, 64])
                            A4 = tA[r][:, :].rearrange("p (a b d) -> p a b d", a=2, b=2)
                            B4 = tB[r][:, :].rearrange("p (a b d) -> p a b d", a=2, b=2)
                            R4 = rq[r][:, :].rearrange("p (a b d) -> p a b d", a=2, b=2)
                            P.op("dve", I("tensor_tensor", out=A4, in0=ps4, in1=cosb, op=ALU.mult),
                                 reads=[pqn, "cst"], writes=["tA%d" % r])
                            P.op("dve", I("tensor_tensor", out=B4, in0=ps4, in1=sinb, op=ALU.mult),
                                 reads=[pqn, "cst"], writes=["tB%d" % r])
                            P.op("dve", I("tensor_tensor", out=R4[:, :, 0, :], in0=A4[:, :, 0, :], in1=B4[:, :, 1, :], op=ALU.subtract),
                                 reads=["tA%d" % r, "tB%d" % r], writes=["rq%d" % r])
                            P.op("dve", I("tensor_tensor", out=R4[:, :, 1, :], in0=A4[:, :, 1, :], in1=B4[:, :, 0, :], op=ALU.add),
                                 reads=["tA%d" % r, "tB%d" % r], writes=["rq%d" % r])
                            rqk = rq[r][:, 128:256]
                            P.op("dve", I("tensor_scalar_mul", out=B_["kd"][:, i, :], in0=rqk, scalar1=C("kdec", h, h + 1)),
                                 reads=["rq%d" % r, "cst"], writes=[bn + "kd%d" % i])
                            P.op("dve", I("tensor_scalar_mul", out=kdl[r][:, :], in0=rqk, scalar1=C("kdecL", i * H + h, i * H + h + 1)),
                                 reads=["rq%d" % r, "cst"], writes=["kdl%d" % r])
                            tpk, tpn = tp_next()
                            P.op("pe", [(I("transpose", out=tpk[:, a, :], in_=rq[r][:, a * 128:(a + 1) * 128], identity=ident[:, :])) for a in range(2)],
                                 reads=["rq%d" % r, "ident"], writes=[tpn])
                            P.op("dve", I("tensor_copy", out=B_["qT"][:, tok], in_=tpk[:, 0, :]), reads=[tpn], writes=[bn + "qT%d" % i])
                            P.op("dve", I("tensor_tensor", out=B_["qdT"][:, tok], in0=tpk[:, 0, :], in1=C("QD", h * 128, (h + 1) * 128), op=ALU.mult),
                                 reads=[tpn, "cst"], writes=[bn + "qdT%d" % i])
                            P.op("dve", I("tensor_copy", out=B_["kT"][:, tok], in_=tpk[:, 1, :]), reads=[tpn], writes=[bn + "kT%d" % i])
                            P.op("act", I("copy", out=B_["v"][:, i, :], in_=pv[:, 0:256]), reads=[pvn], writes=[bn + "v%d" % i])
                            P.op("act", I("activation", out=B_["sg"][:, i, :], in_=pv[:, 256:512], func=AF.Silu), reads=[pvn], writes=[bn + "sg%d" % i])
                            P.op("pe", I("matmul", pl[:, 0:256], lhsT=kdl[r][:, :], rhs=B_["v"][:, i, :], start=(i == 0), stop=(i == NT - 1)),
                                 reads=["kdl%d" % r, bn + "v%d" % i], writes=[pln])
                        P.op("dve", I("tensor_copy", out=Lsb[:], in_=pl[:, 0:256]), reads=[pln], writes=["Lsb"])
                        P.dma("sp", I("dma_start", out=st_b[h][:, :], in_=Lsb[:]), reads=["Lsb"], writes=["st_b%d" % h])
                        P.cc("pool", I("collective_compute", "AllGather", ALU.bypass, replica_groups=[list(range(NCORES))],
                                                                   ins=[st_b[h][:, :]], outs=[st_g[h][:, :]]),
                             reads=["st_b%d" % h], writes=["st_g%d" % h])

                    def retention(h):
                        B_ = hb[h % 2]
                        bn = "h%d_" % (h % 2)
                        for part in range(2):
                            P.dma("sp", I("dma_start",
                                out=Gs[:], in_=st_g[h][part * 512:(part + 1) * 512, :].rearrange("(c p) e -> p c e", p=128)),
                                reads=["st_g%d" % h], writes=["Gs"])
                            for cc in range(4):
                                c = part * 4 + cc
                                if c == 0:
                                    P.op("dve", I("tensor_scalar_mul", out=Sf[:], in0=Gs[:, cc, :], scalar1=C("coef", c * H + h, c * H + h + 1)),
                                         reads=["Gs", "cst"], writes=["Sf"])
                                else:
                                    P.op("dve", I("scalar_tensor_tensor", out=Sf[:], in0=Gs[:, cc, :], scalar=C("coef", c * H + h, c * H + h + 1),
                                                                                           in1=Sf[:], op0=ALU.mult, op1=ALU.add),
                                         reads=["Gs", "cst", "Sf"], writes=["Sf"])
                        P.op("act", I("copy", out=Sb[0][:], in_=Sf[:]), reads=["Sf"], writes=["Sb0"])
                        for i in range(NT):
                            tok = slice(i * 128, (i + 1) * 128)
                            s_ = i % 2
                            pst, pstn = ps_next()
                            P.op("pe", I("matmul", pst[:, 0:128], lhsT=B_["kT"][:, tok], rhs=B_["qT"][:, tok], start=True, stop=True),
                                 reads=[bn + "kT%d" % i, bn + "qT%d" % i], writes=[pstn])
                            pu, pun = ps_next()
                            P.op("pe", I("matmul", pu[:, 0:256], lhsT=B_["kd"][:, i, :], rhs=B_["v"][:, i, :], start=True, stop=True),
                                 reads=[bn + "kd%d" % i, bn + "v%d" % i], writes=[pun])
                            P.op("dve", I("tensor_tensor", out=sTs[s_][:], in0=pst[:, 0:128], in1=C("DT", h * 128, (h + 1) * 128), op=ALU.mult),
                                 reads=[pstn, "cst"], writes=["sTs%d" % s_])
                            po, pon = ps_next()
                            P.op("pe", [I("matmul", po[:, 0:256], lhsT=sTs[s_][:], rhs=B_["v"][:, i, :], start=True, stop=False),
                                        I("matmul", po[:, 0:256], lhsT=B_["qdT"][:, tok], rhs=Sb[s_][:], start=False, stop=True)],
                                 reads=["sTs%d" % s_, bn + "v%d" % i, bn + "qdT%d" % i, "Sb%d" % s_], writes=[pon])
                            if i < NT - 1:
                                P.op("dve", I("scalar_tensor_tensor", out=Sf[:], in0=Sf[:], scalar=C("cd2", h, h + 1), in1=pu[:, 0:256], op0=ALU.mult, op1=ALU.add),
                                     reads=["Sf", pun, "cst"], writes=["Sf"])
                                P.op("act", I("copy", out=Sb[1 - s_][:], in_=Sf[:]), reads=["Sf"], writes=["Sb%d" % (1 - s_)])
                            gidx = cnt3["gn"] % 2
                            cnt3["gn"] += 1
                            gc = gidx * (NT * 4) + (i % NT) * 4
                            gsn = "gst%d_%d" % (gidx, i)
                            c_sum, c_sq, c_rs, c_nb = (gst[:, gc + k:gc + k + 1] for k in range(4))
                            P.op("dve", I("memset", gst[:, gc:gc + 4], 0.0), writes=[gsn])
                            P.op("act", I("activation", out=gjunk[:], in_=po[:, 0:256], func=AF.Copy, accum_out=c_sum),
                                 reads=[pon, gsn], writes=["gjunk", gsn])
                            P.op("act", I("activation", out=gjunk[:], in_=po[:, 0:256], func=AF.Square, accum_out=c_sq),
                                 reads=[pon, gsn], writes=["gjunk", gsn])
                            P.op("dve", I("tensor_scalar_mul", out=c_sum, in0=c_sum, scalar1=1.0 / 256), reads=[gsn], writes=[gsn])
                            P.op("dve", I("tensor_tensor", out=c_nb, in0=c_sum, in1=c_sum, op=ALU.mult), reads=[gsn], writes=[gsn])
                            P.op("dve", I("scalar_tensor_tensor", out=c_rs, in0=c_sq, scalar=1.0 / 256, in1=c_nb, op0=ALU.mult, op1=ALU.subtract),
                                 reads=[gsn], writes=[gsn])
                            P.op("dve", I("tensor_scalar_add", out=c_rs, in0=c_rs, scalar1=1e-5), reads=[gsn], writes=[gsn])
                            P.op("act", I("sqrt", out=c_rs, in_=c_rs), reads=[gsn], writes=[gsn])
                            P.op("dve", I("reciprocal", out=c_rs, in_=c_rs), reads=[gsn], writes=[gsn])
                            P.op("dve", I("scalar_tensor_tensor", out=c_nb, in0=c_sum, scalar=-1.0, in1=c_rs, op0=ALU.mult, op1=ALU.mult),
                                 reads=[gsn], writes=[gsn])
                            P.op("act", I("activation", out=on[gidx][:], in_=po[:, 0:256], func=AF.Identity, bias=c_nb, scale=c_rs),
                                 reads=[pon, gsn], writes=["on%d" % gidx])
                            P.op("dve", I("tensor_tensor", out=rtm[gidx][:], in0=on[gidx][:], in1=B_["sg"][:, i, :], op=ALU.mult),
                                 reads=["on%d" % gidx, bn + "sg%d" % i], writes=["rtm%d" % gidx])
                            tpk, tpn = tp_next()
                            P.op("pe", [(I("transpose", out=tpk[:, a, :], in_=rtm[gidx][:, a * 128:(a + 1) * 128], identity=ident[:, :])) for a in range(2)],
                                 reads=["rtm%d" % gidx, "ident"], writes=[tpn])
                            gb = C("gng", h * 2, h * 2 + 2).unsqueeze(2).to_broadcast([128, 2, 128])
                            P.op("dve", I("tensor_tensor", out=rT[:, h * 2:h * 2 + 2, tok], in0=tpk[:, 0:2, :], in1=gb, op=ALU.mult),
                                 reads=[tpn, "cst"], writes=["rT_%d" % i])

                    if P3ONLY:
                        P.muted = False
                        for q4 in range(4):
                            P.dma("pool", I("dma_start", out=R64b[:, q4 * 4096:(q4 + 1) * 4096], in_=uT_in[:, q4 * 4096:(q4 + 1) * 4096]),
                                  writes=["uT_%d" % i for i in range(NT)])
                    inproj(0)
                    for h in range(HD):
                        if h + 1 < HD:
                            inproj(h + 1)
                        if "noret" not in P3FLAGS:
                            retention(h)
                    P.barrier()
                    stop_here(3)

                with ExitStack() as p4:
                    sig = [sb("sigb%d" % i, [128, 512], F32, p4) for i in range(2)]
                    tmpm = [sb("tmpm%d" % i, [128, 512], F32, p4) for i in range(2)]
                    for f in range(16):
                        wpv, wpn = w_get(128)
                        wgv, wgn = w_get(128)
                        for half in range(2):
                            pa, pan = ps_next()
                            P.op("pe", [(I("matmul", pa[:, :], lhsT=wpv[:, kt, :], rhs=rT[:, kt, half * 512:(half + 1) * 512],
                                                                                  start=(kt == 0), stop=(kt == KT - 1))) for kt in range(KT)],
                                 reads=[wpn] + rT_b(half), writes=[pan])
                            pb, pbn = ps_next()
                            P.op("pe", [(I("matmul", pb[:, :], lhsT=wgv[:, kt, :], rhs=uT[:, kt, half * 512:(half + 1) * 512],
                                                                                  start=(kt == 0), stop=(kt == KT - 1))) for kt in range(KT)],
                                 reads=[wgn] + uT_b(half), writes=[pbn])
                            sg_, sgn = sig[half], "sigb%d" % half
                            tm_, tmn = tmpm[half], "tmpm%d" % half
                            mname = "mT_%d_%d" % (f, half)
                            P.op("act", I("activation", out=sg_[:], in_=pb[:, :], func=AF.Sigmoid), reads=[pbn], writes=[sgn])
                            P.op("dve", I("tensor_tensor", out=tm_[:], in0=pa[:, :], in1=sg_[:], op=ALU.mult),
                                 reads=[pan, sgn], writes=[tmn])
                            P.op("dve", I("tensor_tensor",
                                out=mT[:, f, half * 512:(half + 1) * 512], in0=mT[:, f, half * 512:(half + 1) * 512], in1=tm_[:], op=ALU.add),
                                reads=[tmn, mname], writes=[mname])
                    P.barrier()
                    stop_here(4)

                mnames = ["mT_%d_%d" % (f, hh) for f in range(16) for hh in range(2)]
                for i in range(NT):
                    P.dma("sp", I("dma_start", out=hres[:, i, :], in_=x_d[i * 128:(i + 1) * 128, :]), writes=["h_%d" % i])
                for n in range(8):
                    wo, won = w_get(256)
                    for i in range(NT):
                        tok = slice(i * 128, (i + 1) * 128)
                        pt, pn = ps_next()
                        P.op("pe", [(I("matmul", pt[:, 0:256], lhsT=mT[:, kt, tok], rhs=wo[:, kt, :],
                                                                              start=(kt == 0), stop=(kt == KT - 1))) for kt in range(KT)],
                             reads=[won] + mnames, writes=[pn])
                        P.op("dve", I("tensor_tensor", out=hres[:, i, n * 256:(n + 1) * 256], in0=hres[:, i, n * 256:(n + 1) * 256],
                                                                             in1=pt[:, 0:256], op=ALU.add),
                             reads=[pn, "h_%d" % i], writes=["h_%d" % i])
                P.barrier()
                stop_here(5)

            with ExitStack() as ff:
                u2T = sb("u2T", [128, KT, TOK], BF16, ff)
                u2hT = sb("u2hT", [128, KT, 16], BF16, ff)
                aT = sb("aT", [128, DN_G, TOK], BF16, ff)
                with ExitStack() as f1:
                    xn = [sb("xnb%d" % i, [128, D], BF16, f1) for i in range(2)]
                    junk = sb("junkb", [128, D], BF16, f1)
                    hl_f = sb("hl_f", [128, 256], F32, f1)
                    G2 = sb("G2", [128, NCORES, 256], F32, f1)
                    hacc = sb("hacc", [128, 256], F32, f1)
                    for i in range(NT):
                        dst_fn = (lambda half, i=i: u2T[:, half * 8:(half + 1) * 8, i * 128:(i + 1) * 128])
                        norm_transpose(hres[:, i, :], 128, ["h_%d" % i], xn[i % 2], "xnb%d" % (i % 2), junk, 16 + i, "g2", dst_fn, "u2T_%d" % i)
                    P.op("dve", I("tensor_copy", out=hl_f[:, :].rearrange("p (k t) -> p k t", t=16), in_=u2T[:, :, TOK - 16:TOK]),
                         reads=["u2T_7"], writes=["hl_f"])
                    P.dma("sp", I("dma_start", out=hl_b[:, :], in_=hl_f[:]), reads=["hl_f"], writes=["hl_b"])
                    P.cc("pool", I("collective_compute", "AllGather", ALU.bypass, replica_groups=[list(range(NCORES))],
                                                               ins=[hl_b[:, :]], outs=[hl_g[:, :]]),
                         reads=["hl_b"], writes=["hl_g"])
                    P.dma("sp", I("dma_start", out=G2[:], in_=hl_g.rearrange("(c p) e -> p c e", p=128)), reads=["hl_g"], writes=["G2"])
                    for c in range(NCORES):
                        if c == 0:
                            P.op("dve", I("tensor_scalar_mul", out=hacc[:], in0=G2[:, c, :], scalar1=C("sel", c, c + 1)),
                                 reads=["G2", "cst"], writes=["hacc"])
                        else:
                            P.op("dve", I("scalar_tensor_tensor", out=hacc[:], in0=G2[:, c, :], scalar=C("sel", c, c + 1), in1=hacc[:],
                                                                              op0=ALU.mult, op1=ALU.add),
                                 reads=["G2", "cst", "hacc"], writes=["hacc"])
                    P.op("dve", I("tensor_copy", out=u2hT[:, :, :], in_=hacc[:, :].rearrange("p (k t) -> p k t", t=16)), reads=["hacc"], writes=["u2hT"])
                    P.barrier()
                    stop_here(6)

                def u2_b(half):
                    return ["u2T_%d" % i for i in range(half * 4, half * 4 + 4)]

                with ExitStack() as f2:
                    gsb = [sb("gsb%d" % i, [128, 2 + TOK], F32, f2) for i in range(2)]
                    cvt = sb("cvt", [128, TOK], F32, f2)
                    sl = sb("sl", [128, TOK], F32, f2)
                    for (j0, g) in dn_groups():
                        for jj in range(g):
                            j = j0 + jj
                            wu, wun = w_get(256)
                            gs_, gsn = gsb[j % 2], "gsb%d" % (j % 2)
                            pg = []
                            for half in range(2):
                                pt, pn = ps_next()
                                P.op("pe", [(I("matmul", pt[:, :], lhsT=wu[:, kt, 0:128], rhs=u2T[:, kt, half * 512:(half + 1) * 512],
                                                                                      start=(kt == 0), stop=(kt == KT - 1))) for kt in range(KT)],
                                     reads=[wun] + u2_b(half), writes=[pn])
                                pg.append((pt, pn))
                            ph, phn = ps_next()
                            P.op("pe", [(I("matmul", ph[:, 0:16], lhsT=wu[:, kt, 0:128], rhs=u2hT[:, kt, :],
                                                                         start=(kt == 0), stop=(kt == KT - 1))) for kt in range(KT)],
                                 reads=[wun, "u2hT"], writes=[phn])
                            pu = []
                            for half in range(2):
                                pt, pn = ps_next()
                                P.op("pe", [(I("matmul", pt[:, :], lhsT=wu[:, kt, 128:256], rhs=u2T[:, kt, half * 512:(half + 1) * 512],
                                                                                      start=(kt == 0), stop=(kt == KT - 1))) for kt in range(KT)],
                                     reads=[wun] + u2_b(half), writes=[pn])
                                pu.append((pt, pn))
                            P.op("act", I("copy", out=gs_[:, 0:2], in_=ph[:, 14:16]), reads=[phn], writes=[gsn])
                            for half in range(2):
                                P.op("act", I("copy", out=gs_[:, 2 + half * 512:2 + (half + 1) * 512], in_=pg[half][0][:, :]),
                                     reads=[pg[half][1]], writes=[gsn])
                            cwj = lambda k, j=j: C("cw", j * 3 + k, j * 3 + k + 1)
                            P.op("act", I("activation", out=cvt[:], in_=gs_[:, 2:2 + TOK], func=AF.Identity, bias=C("cb", j, j + 1), scale=cwj(2)),
                                 reads=[gsn, "cst"], writes=["cvt"])
                            P.op("dve", I("scalar_tensor_tensor", out=cvt[:], in0=gs_[:, 1:1 + TOK], scalar=cwj(1), in1=cvt[:], op0=ALU.mult, op1=ALU.add),
                                 reads=[gsn, "cst", "cvt"], writes=["cvt"])
                            P.op("dve", I("scalar_tensor_tensor", out=cvt[:], in0=gs_[:, 0:TOK], scalar=cwj(0), in1=cvt[:], op0=ALU.mult, op1=ALU.add),
                                 reads=[gsn, "cst", "cvt"], writes=["cvt"])
                            P.op("act", I("activation", out=sl[:], in_=cvt[:], func=AF.Silu), reads=["cvt"], writes=["sl"])
                            for half in range(2):
                                P.op("dve", I("tensor_tensor",
                                    out=aT[:, jj, half * 512:(half + 1) * 512], in0=sl[:, half * 512:(half + 1) * 512], in1=pu[half][0][:, :], op=ALU.mult),
                                    reads=["sl", pu[half][1]], writes=["aT_%d" % jj])
                        for n in range(4):
                            wd_, wdn = w_get(512)
                            for i in range(NT):
                                tok = slice(i * 128, (i + 1) * 128)
                                pt, pn = ps_next()
                                P.op("pe", [(I("matmul", pt[:, :], lhsT=aT[:, jj, tok], rhs=wd_[:, jj, :],
                                                                                      start=(jj == 0), stop=(jj == g - 1))) for jj in range(g)],
                                     reads=[wdn] + ["aT_%d" % jj for jj in range(g)], writes=[pn])
                                P.op("dve", I("tensor_tensor", out=hres[:, i, n * 512:(n + 1) * 512], in0=hres[:, i, n * 512:(n + 1) * 512],
                                                                                     in1=pt[:, :], op=ALU.add),
                                     reads=[pn, "h_%d" % i], writes=["h_%d" % i])
                    P.barrier()
                    stop_here(7)

                with ExitStack() as f3:
                    gfin = sb("gfin", [128, D], F32, f3)
                    junk = sb("junkc", [128, D], BF16, f3)
                    P.dma("sp", I("dma_start", out=gfin[:], in_=gfin_d[:, :]), writes=["gfin"])
                    for i in range(NT):
                        sc = stat[:, 32 + i:33 + i]
                        sn = "st%d" % (32 + i)
                        hn = "h_%d" % i
                        P.op("act", I("activation", out=junk[:], in_=hres[:, i, :], func=AF.Square, accum_out=sc),
                             reads=[hn, "stat"], writes=["junkc", sn])
                        P.op("dve", I("tensor_scalar", out=sc, in0=sc, scalar1=1.0 / D, scalar2=1e-6, op0=ALU.mult, op1=ALU.add), reads=[sn], writes=[sn])
                        P.op("act", I("sqrt", out=sc, in_=sc), reads=[sn], writes=[sn])
                        P.op("dve", I("reciprocal", out=sc, in_=sc), reads=[sn], writes=[sn])
                        P.op("dve", I("scalar_tensor_tensor", out=hres[:, i, :], in0=hres[:, i, :], scalar=sc, in1=gfin[:], op0=ALU.mult, op1=ALU.mult),
                             reads=[hn, sn, "gfin"], writes=[hn])
                        P.dma("sp", I("dma_start", out=out_d[i * 128:(i + 1) * 128, :], in_=hres[:, i, :]), reads=[hn])
                    P.final_wait("sp")

        P.replay(block, sems)
    return nc


def _blk(W, c0, width):
    K = W.shape[0]
    return np.ascontiguousarray(W[:, c0:c0 + width].reshape(K // 128, 128, width).transpose(1, 0, 2)).reshape(128, -1)


def _fm(vec):
    return np.ascontiguousarray(vec.reshape(-1, 128).T)


def _const_tables(core):
    s = core % 4
    t0 = s * TOK
    f64 = np.float64
    tab = np.zeros((128, CST_W), np.float32)

    def put(name, arr):
        o, w = CST_OFF[name]
        tab[:, o:o + w] = np.asarray(arr, np.float32).reshape(128, w)

    lg = np.log(1.0 - 2.0 ** (-5.0 - np.arange(H, dtype=f64)))
    inv = 10000.0 ** (-np.arange(64, dtype=np.float32) / 64)
    pos = (t0 + np.arange(TOK)).astype(np.float32)
    ang = pos[:, None] * inv[None, :]
    cosv = np.cos(ang).astype(f64).reshape(NT, 128, 64).transpose(1, 0, 2)
    sinv = np.sin(ang).astype(f64).reshape(NT, 128, 64).transpose(1, 0, 2)
    sc = np.array([1.0, 128.0 ** -0.5])
    put("cos", cosv[:, :, None, :] * sc[None, None, :, None])
    put("sin", sinv[:, :, None, :] * sc[None, None, :, None])
    ii = np.arange(128)
    i_ = ii[None, :]
    j_ = ii[:, None]
    same = (i_ // 64) == (j_ // 64)
    later = (j_ < 64) & (i_ >= 64)
    DT = np.zeros((128, H, 128))
    for h in range(H):
        DT[:, h, :] = np.where(same, np.exp(np.abs(i_ - j_) * lg[h]), np.where(later, np.exp((i_ - j_) * lg[h]), 0.0))
    put("DT", DT)
    put("QD", np.broadcast_to(np.exp((ii[None, None, :] + 1) * lg[None, :, None]), (128, H, 128)))
    put("kdec", np.exp((127 - ii)[:, None] * lg[None, :]))
    til = np.arange(NT)
    put("kdecL", np.exp((1023 - til[None, :, None] * 128 - ii[:, None, None]) * lg[None, None, :]))
    put("cd2", np.broadcast_to(np.exp(128 * lg)[None, :], (128, H)))
    coef = np.zeros((NCORES, H))
    for c in range(NCORES):
        if c // 4 == core // 4 and c % 4 < s:
            coef[c] = np.exp(1024.0 * (s - 1 - c % 4) * lg)
    put("coef", np.broadcast_to(coef[None], (128, NCORES, H)))
    invc = np.zeros((4, 16))
    for g, w in enumerate((2, 4, 8, 16)):
        invc[g] = 1.0 / np.minimum(t0 + np.arange(16) + 1, w)
    put("invc", np.broadcast_to(invc[None], (128, 4, 16)))
    sel = np.zeros(NCORES)
    if s > 0:
        sel[core - 1] = 1.0
    put("sel", np.broadcast_to(sel[None], (128, NCORES)))
    return tab, put


def _prep_inputs(x, norm_mix, w_in, w_pool_group, pool_scale, w_pool_proj, ret_gn_gain, w_ret_proj, w_out,
                 norm_ffn, w_up, conv_w, conv_b, w_down, norm_final):
    f = np.float32
    x = np.asarray(x, f)
    w_in = np.asarray(w_in, f)[0]
    w_up_ = np.asarray(w_up, f)[0]
    w_down_ = np.asarray(w_down, f)[0]
    shared = {}
    shared["w_pin"] = np.stack([_blk(w_in, b * 128, 128) for b in range(8)])
    shared["w_gp"] = np.stack([_blk(w_in, 7168 + b * 128, 128) for b in range(16)])
    shared["w_gr"] = np.stack([_blk(w_in, 9216 + b * 128, 128) for b in range(16)])
    qk = []
    for h in range(H):
        cols = np.concatenate([w_in[:, 1024 + h * 128:1024 + (h + 1) * 128], w_in[:, 2048 + h * 128:2048 + (h + 1) * 128]], axis=1)
        qk.append(_blk(cols, 0, 256))
    shared["w_qk"] = np.stack(qk)
    shared["w_v"] = np.stack([_blk(w_in, 3072 + h * 256, 256) for h in range(H)])
    shared["w_g"] = np.stack([_blk(w_in, 5120 + h * 256, 256) for h in range(H)])
    wpg = np.asarray(w_pool_group, f)[0]
    shared["w_pg"] = np.stack([_blk(wpg[g], 0, 256) for g in range(4)])
    wpp = np.asarray(w_pool_proj, f)[0]
    shared["w_pp"] = np.stack([_blk(wpp, b * 128, 128) for b in range(16)])
    wrp = np.asarray(w_ret_proj, f)[0]
    shared["w_rp"] = np.stack([_blk(wrp, b * 128, 128) for b in range(16)])
    wo = np.asarray(w_out, f)[0]
    shared["w_o"] = np.stack([_blk(wo, n * 256, 256) for n in range(8)])
    ups = []
    for j in range(NJ):
        cols = np.concatenate([w_up_[:, j * 128:(j + 1) * 128], w_up_[:, DFF + j * 128:DFF + (j + 1) * 128]], axis=1)
        ups.append(_blk(cols, 0, 256))
    shared["w_up"] = np.stack(ups)
    shared["w_dn"] = np.stack([_blk(w_down_, n * 512, 512) for n in range(4)])
    shared["gfin"] = np.ascontiguousarray(np.broadcast_to(np.asarray(norm_final, f)[None, :], (128, D)))
    cw = np.asarray(conv_w, f)[0]
    cwt = np.ascontiguousarray(cw.reshape(3, NJ, 128).transpose(2, 1, 0))
    cbt = _fm(np.asarray(conv_b, f)[0])
    in_maps = []
    for c in range(NCORES):
        b, s = c // 4, c % 4
        t0 = s * TOK
        tab, put = _const_tables(c)
        put("g1", _fm(np.asarray(norm_mix, f)[0]))
        put("g2", _fm(np.asarray(norm_ffn, f)[0]))
        put("pscale", _fm(np.asarray(pool_scale, f)[0]))
        put("gng", _fm(np.asarray(ret_gn_gain, f)[0]))
        put("cw", cwt)
        put("cb", cbt)
        m = dict(shared)
        m["x"] = np.ascontiguousarray(x[b, t0:t0 + TOK])
        m["xh"] = np.ascontiguousarray(x[b, t0 - 16:t0]) if s > 0 else np.zeros((16, D), f)
        m["cst"] = tab
        in_maps.append(m)
    return in_maps


_NC_CACHE = {}


def kernel(**inputs):
    in_maps = _prep_inputs(**inputs)
    if "nc" not in _NC_CACHE:
        _NC_CACHE["nc"] = build_nc()
    res = run_bass_kernel_spmd(_NC_CACHE["nc"], in_maps, core_ids=list(range(NCORES)))
    out = np.empty((2, 4 * TOK, D), np.float32)
    for c in range(NCORES):
        out[c // 4, (c % 4) * TOK:(c % 4 + 1) * TOK] = res.results[c]["out"]
    return out
```
